# Optimizing a Trainium2 kernel written in Bass

```python
import math
import jax, jax.numpy as jnp
from jax import lax
import numpy as np

D_MODEL = 4096
BATCH = 4
SEQ = 2048
DEPTH = 1

CHUNK = 64
D_MIX = D_MODEL
D_SSM = D_MIX // 2
D_CONV = D_MIX - D_SSM
SSM_GROUP = 16
N_SSM_GROUPS = D_SSM // SSM_GROUP
SSM_STATE = 64
CONV_WIDTH = 31
D_FF = -(-(8 * D_MODEL) // (3 * 256)) * 256
D_IN = D_SSM + 2 * D_CONV
EPS = 1e-6
DT_MIN = 1e-3
DT_MAX = 1e-1

kernel_name = "hybrid_s5_conformer_conv_sandwich_block"


def rmsnorm(x, g):
    xf = x.astype(jnp.float32)
    y = xf * lax.rsqrt(jnp.mean(xf * xf, axis=-1, keepdims=True) + EPS)
    return (y * g.astype(jnp.float32)).astype(x.dtype)


def layernorm(x, g, b):
    xf = x.astype(jnp.float32)
    mu = jnp.mean(xf, axis=-1, keepdims=True)
    xc = xf - mu
    y = xc * lax.rsqrt(jnp.mean(xc * xc, axis=-1, keepdims=True) + EPS)
    return (y * g.astype(jnp.float32) + b.astype(jnp.float32)).astype(x.dtype)


def _scan_combine(left, right):
    a1r, a1i, b1r, b1i = left
    a2r, a2i, b2r, b2i = right
    return (a2r * a1r - a2i * a1i,
            a2r * a1i + a2i * a1r,
            a2r * b1r - a2i * b1i + b2r,
            a2r * b1i + a2i * b1r + b2i)


def s5_mixer(u, a_re, a_im, log_dt, b_re, b_im, c_re, c_im, d_skip, w_glu, b_glu):
    bsz, seq, _ = u.shape
    f32 = jnp.float32
    uf = u.astype(f32).reshape(bsz, seq, N_SSM_GROUPS, SSM_GROUP)
    lam_re, lam_im = a_re.astype(f32), a_im.astype(f32)
    dt = jnp.exp(log_dt.astype(f32))[:, None]
    z_re, z_im = lam_re * dt, lam_im * dt
    mag = jnp.exp(z_re)
    abar_re, abar_im = mag * jnp.cos(z_im), mag * jnp.sin(z_im)
    den = lam_re * lam_re + lam_im * lam_im
    num_re, num_im = abar_re - 1.0, abar_im
    coef_re = (num_re * lam_re + num_im * lam_im) / den
    coef_im = (num_im * lam_re - num_re * lam_im) / den
    bu_re = jnp.einsum('blgh,gph->blgp', uf, b_re.astype(f32))
    bu_im = jnp.einsum('blgh,gph->blgp', uf, b_im.astype(f32))
    in_re = coef_re * bu_re - coef_im * bu_im
    in_im = coef_re * bu_im + coef_im * bu_re
    ar = jnp.broadcast_to(abar_re, (1, seq) + abar_re.shape)
    ai = jnp.broadcast_to(abar_im, (1, seq) + abar_im.shape)
    _, _, s_re, s_im = lax.associative_scan(_scan_combine, (ar, ai, in_re, in_im), axis=1)
    y = (jnp.einsum('blgp,ghp->blgh', s_re, c_re.astype(f32))
         - jnp.einsum('blgp,ghp->blgh', s_im, c_im.astype(f32)))
    y = y + d_skip.astype(f32).reshape(N_SSM_GROUPS, SSM_GROUP) * uf
    y = jax.nn.gelu(y.reshape(bsz, seq, D_SSM)).astype(u.dtype)
    return y * jax.nn.sigmoid(y @ w_glu + b_glu)


def conformer_conv(val, gate, w_dw, b_dw, ln_g, ln_b):
    h = val * jax.nn.sigmoid(gate)
    h = lax.conv_general_dilated(
        h, w_dw[:, None, :].astype(h.dtype), window_strides=(1,),
        padding=[(CONV_WIDTH - 1, 0)],
        dimension_numbers=('NWC', 'WIO', 'NWC'),
        feature_group_count=D_CONV) + b_dw
    h = layernorm(h, ln_g, ln_b)
    return jax.nn.silu(h)


def setup_inputs(seed: int = 0) -> dict:
    key = jax.random.key(seed)
    ks = jax.random.split(key, 32)
    f32 = jnp.float32
    L = DEPTH

    def nrm(k, shape, scale):
        return jax.random.normal(k, shape, f32) * scale

    def gain(k, dim):
        return 1.0 + 0.01 * jax.random.normal(k, (L, dim), f32)

    x = jax.random.normal(ks[0], (BATCH, SEQ, D_MODEL), f32)
    n_idx = jnp.arange(SSM_STATE, dtype=f32)
    a_re = -0.5 + 0.01 * jax.random.normal(ks[3], (L, N_SSM_GROUPS, SSM_STATE), f32)
    a_im = math.pi * n_idx[None, None, :] + 0.01 * jax.random.normal(ks[4], (L, N_SSM_GROUPS, SSM_STATE), f32)
    log_dt = jax.random.uniform(ks[5], (L, N_SSM_GROUPS), f32, math.log(DT_MIN), math.log(DT_MAX))
    return {
        "x": x,
        "ln_pre_mix": gain(ks[1], D_MODEL),
        "w_in": nrm(ks[2], (L, D_MODEL, D_IN), D_MODEL ** -0.5),
        "ssm_a_re": a_re,
        "ssm_a_im": a_im,
        "ssm_log_dt": log_dt,
        "ssm_b_re": nrm(ks[6], (L, N_SSM_GROUPS, SSM_STATE, SSM_GROUP), (2.0 * SSM_GROUP) ** -0.5),
        "ssm_b_im": nrm(ks[7], (L, N_SSM_GROUPS, SSM_STATE, SSM_GROUP), (2.0 * SSM_GROUP) ** -0.5),
        "ssm_c_re": nrm(ks[8], (L, N_SSM_GROUPS, SSM_GROUP, SSM_STATE), (2.0 * SSM_STATE) ** -0.5),
        "ssm_c_im": nrm(ks[9], (L, N_SSM_GROUPS, SSM_GROUP, SSM_STATE), (2.0 * SSM_STATE) ** -0.5),
        "ssm_d": nrm(ks[10], (L, D_SSM), 1.0),
        "ssm_w_glu": nrm(ks[11], (L, D_SSM, D_SSM), D_SSM ** -0.5),
        "ssm_b_glu": nrm(ks[12], (L, D_SSM), 0.01),
        "conv_w_dw": nrm(ks[13], (L, CONV_WIDTH, D_CONV), CONV_WIDTH ** -0.5),
        "conv_b_dw": nrm(ks[14], (L, D_CONV), 0.01),
        "conv_ln_g": gain(ks[15], D_CONV),
        "conv_ln_b": nrm(ks[16], (L, D_CONV), 0.01),
        "norm_ssm_out": gain(ks[17], D_SSM),
        "norm_conv_out": gain(ks[18], D_CONV),
        "w_out": nrm(ks[19], (L, D_MIX, D_MODEL), D_MIX ** -0.5),
        "ln_post_mix": gain(ks[20], D_MODEL),
        "ln_pre_ffn": gain(ks[21], D_MODEL),
        "w_gate": nrm(ks[22], (L, D_MODEL, D_FF), D_MODEL ** -0.5),
        "w_up": nrm(ks[23], (L, D_MODEL, D_FF), D_MODEL ** -0.5),
        "w_down": nrm(ks[24], (L, D_FF, D_MODEL), D_FF ** -0.5),
        "ln_post_ffn": gain(ks[25], D_MODEL),
    }


def reference(x, ln_pre_mix, w_in, ssm_a_re, ssm_a_im, ssm_log_dt, ssm_b_re, ssm_b_im,
              ssm_c_re, ssm_c_im, ssm_d, ssm_w_glu, ssm_b_glu, conv_w_dw, conv_b_dw,
              conv_ln_g, conv_ln_b, norm_ssm_out, norm_conv_out, w_out, ln_post_mix,
              ln_pre_ffn, w_gate, w_up, w_down, ln_post_ffn):
    for l in range(DEPTH):
        h = rmsnorm(x, ln_pre_mix[l])
        p = h @ w_in[l]
        u_ssm = p[..., :D_SSM]
        conv_val = p[..., D_SSM:D_SSM + D_CONV]
        conv_gate = p[..., D_SSM + D_CONV:]
        y_ssm = s5_mixer(u_ssm, ssm_a_re[l], ssm_a_im[l], ssm_log_dt[l], ssm_b_re[l], ssm_b_im[l],
                         ssm_c_re[l], ssm_c_im[l], ssm_d[l], ssm_w_glu[l], ssm_b_glu[l])
        y_conv = conformer_conv(conv_val, conv_gate, conv_w_dw[l], conv_b_dw[l],
                                conv_ln_g[l], conv_ln_b[l])
        mixed = jnp.concatenate([rmsnorm(y_ssm, norm_ssm_out[l]),
                                 rmsnorm(y_conv, norm_conv_out[l])], axis=-1)
        x = x + rmsnorm(mixed @ w_out[l], ln_post_mix[l])
        h = rmsnorm(x, ln_pre_ffn[l])
        f = (jax.nn.silu(h @ w_gate[l]) * (h @ w_up[l])) @ w_down[l]
        x = x + rmsnorm(f, ln_post_ffn[l])
    return x
```

```python
import math
import contextlib
import numpy as np
import concourse.bass as bass
import concourse.mybir as mybir
from concourse.bass_utils import run_bass_kernel_spmd

F32 = mybir.dt.float32
BF16 = mybir.dt.bfloat16
AF = mybir.ActivationFunctionType
ALU = mybir.AluOpType


class Sched:
    ENGS = ["pe", "act", "dve", "pool", "sp"]

    def __init__(self, nc, es, n_dma_sems=40):
        self.nc = nc
        self.q = {"pe": nc.tensor, "act": nc.scalar, "dve": nc.vector, "pool": nc.gpsimd, "sp": nc.sync}
        self.sem = {n: es.enter_context(nc.semaphore("sem_" + n)) for n in self.ENGS}
        self.count = {n: 0 for n in self.ENGS}
        self.seen = {n: {} for n in self.ENGS}
        self.free_dma_sems = [[es.enter_context(nc.semaphore(f"semd{i}")), 0] for i in range(n_dma_sems)]
        self.free_sw_sems = [[es.enter_context(nc.semaphore(f"semw{i}")), 0] for i in range(8)]
        self.sw_keys = set()
        self.dma_sems = {}
        self.lastw = {}
        self.readers = {}
        self.out_tokens = []
        self.ninst = 0
        import os
        self.limit = int(os.environ.get("S_LIMIT", "100000000"))
        self.nops = 0

    def _waits(self, eng, reads, writes):
        deps = []
        for k in reads:
            t = self.lastw.get(k)
            if t is not None:
                deps.append(t)
        for k in writes:
            t = self.lastw.get(k)
            if t is not None:
                deps.append(t)
            deps.extend(self.readers.get(k, ()))
        seen = self.seen[eng]
        best = {}
        for (sem, val) in deps:
            if eng == "pe" and sem is self.sem["pe"]:
                continue
            key = id(sem)
            if seen.get(key, 0) < val:
                if key not in best or best[key][1] < val:
                    best[key] = (sem, val)
        for key, (sem, val) in best.items():
            seen[key] = val
            self.q[eng].wait_ge(sem, val)
            self.ninst += 1

    def _commit(self, tok, reads, writes):
        for k in writes:
            self.lastw[k] = tok
            self.readers[k] = []
        for k in reads:
            self.readers.setdefault(k, []).append(tok)

    def op(self, eng, fn, reads=(), writes=()):
        self.nops += 1
        if self.nops > self.limit:
            return None
        self._waits(eng, reads, writes)
        ins = fn(self.q[eng])
        self.count[eng] += 1
        tok = (self.sem[eng], self.count[eng])
        ins.then_inc(tok[0], 1)
        self._commit(tok, reads, writes)
        self.ninst += 1
        return tok

    def dma(self, eng, fn, reads=(), writes=(), semkey=None, is_output=False):
        self.nops += 1
        if self.nops > self.limit:
            return None
        if semkey not in self.dma_sems:
            if eng == "pool":
                self.dma_sems[semkey] = self.free_sw_sems.pop()
                self.sw_keys.add(semkey)
            else:
                self.dma_sems[semkey] = self.free_dma_sems.pop()
        s = self.dma_sems[semkey]
        self._waits(eng, reads, writes)
        seen = self.seen[eng]
        if s[1] > 0 and seen.get(id(s[0]), 0) < 16 * s[1]:
            self.q[eng].wait_ge(s[0], 16 * s[1])
            seen[id(s[0])] = 16 * s[1]
        ins = fn(self.q[eng])
        s[1] += 1
        tok = (s[0], 16 * s[1])
        ins.then_inc(tok[0], 16)
        self._commit(tok, reads, writes)
        if is_output:
            self.out_tokens.append(tok)
        self.ninst += 1
        return tok

    def barrier(self, engs=("pe", "act", "dve", "sp", "pool")):
        toks = [(self.sem[n], self.count[n]) for n in self.ENGS if self.count[n] > 0]
        toks += [(s[0], 16 * s[1]) for s in self.dma_sems.values() if s[1] > 0]
        for eng in engs:
            seen = self.seen[eng]
            for (sem, val) in toks:
                if eng == "pe" and sem is self.sem["pe"]:
                    continue
                if seen.get(id(sem), 0) < val:
                    seen[id(sem)] = val
                    self.q[eng].wait_ge(sem, val)
        self.lastw.clear()
        self.readers.clear()
        if set(engs) >= set(self.ENGS):
            for k, v in self.dma_sems.items():
                (self.free_sw_sems if k in self.sw_keys else self.free_dma_sems).append(v)
            self.dma_sems.clear()
            self.sw_keys.clear()

    def finish(self):
        seen = self.seen["sp"]
        best = {}
        for (sem, val) in self.out_tokens:
            k = id(sem)
            if k not in best or best[k][1] < val:
                best[k] = (sem, val)
        for k, (sem, val) in best.items():
            if seen.get(k, 0) < val:
                seen[k] = val
                self.q["sp"].wait_ge(sem, val)


D = 4096; DS = 2048; DC = 2048; DFF = 11008; TOK = 1024; SEQ = 2048; BATCH = 4
G = DS // 16; P = 64; H = 16; GB = min(32, G)
KC = D // 128; NSC = DS // 128; NCC = DC // 128; FC = DFF // 128
EPS = 1e-6


def set_cfg(d, dff):
    global D, DS, DC, DFF, G, GB, KC, NSC, NCC, FC
    D = d; DS = d // 2; DC = d // 2; DFF = dff
    G = DS // 16; GB = min(32, G)
    KC = D // 128; NSC = DS // 128; NCC = DC // 128; FC = DFF // 128


PI = 3.1415925
TWO_PI = 2.0 * math.pi
MAGIC = 12582912.0


def bc_last(ap, n):
    return ap.unsqueeze(2).broadcast_to([ap.shape[0], ap.shape[1], n])


def host_s5_layouts(inp):
    a_re = np.asarray(inp["ssm_a_re"][0], np.float32)
    a_im = np.asarray(inp["ssm_a_im"][0], np.float32)
    ldt = np.asarray(inp["ssm_log_dt"][0], np.float32)
    b_re = np.asarray(inp["ssm_b_re"][0], np.float32)
    b_im = np.asarray(inp["ssm_b_im"][0], np.float32)
    c_re = np.asarray(inp["ssm_c_re"][0], np.float32)
    c_im = np.asarray(inp["ssm_c_im"][0], np.float32)
    d = np.asarray(inp["ssm_d"][0], np.float32)
    out = {}
    a_ri = np.empty((128, 3, G), np.float32)
    a_ri[:, 0, :] = np.tile(a_re.T, (2, 1))
    a_ri[:, 1, :] = np.tile(a_im.T, (2, 1))
    a_ri[:, 2, :] = np.broadcast_to(ldt[None, :], (128, G))
    out["a_ri"] = a_ri
    b_ri = np.empty((128, 2, G, H), np.float32)
    b_ri[:, 0] = np.tile(b_re.transpose(1, 0, 2), (2, 1, 1))
    b_ri[:, 1] = np.tile(b_im.transpose(1, 0, 2), (2, 1, 1))
    out["b_ri"] = b_ri
    c_ri = np.empty((128, G, H), np.float32)
    c_ri[:64] = c_re.transpose(2, 0, 1)
    c_ri[64:] = c_im.transpose(2, 0, 1)
    out["c_ri"] = c_ri
    def gp(m):
        m2 = m.reshape((G // 2, 2) + m.shape[1:])
        m2 = np.moveaxis(m2, 0, 2)
        return np.ascontiguousarray(m2.reshape((128, G // 2) + m.shape[2:]))
    a_gp = np.empty((128, 3, G // 2), np.float32)
    a_gp[:, 0] = gp(a_re)
    a_gp[:, 1] = gp(a_im)
    a_gp[:, 2] = gp(np.broadcast_to(ldt[:, None], (G, P)))
    out["a_gp"] = a_gp
    c_gp = np.empty((128, 2, G // 2, H), np.float32)
    c_gp[:, 0] = gp(c_re.transpose(0, 2, 1))
    c_gp[:, 1] = gp(c_im.transpose(0, 2, 1))
    out["c_gp"] = c_gp
    dg = d.reshape(G, H)
    out["d_sh"] = np.ascontiguousarray(np.tile(dg.T, (8, 1)))
    out["ident"] = np.eye(128, dtype=np.float32)
    return out


def emit_etables(S, nc, es, a_t, n, tag):
    sb = lambda name, shape, dt=F32: es.enter_context(nc.sbuf_tensor(tag + name, shape, dt))
    E = sb("E", [128, 2, 9, n])
    dt_ = sb("dt", [128, n])
    zre = sb("zre", [128, n])
    zim = sb("zim", [128, n])
    ang = sb("ang", [128, n])
    tmp = sb("tmp", [128, n])
    kf = sb("kf", [128, n])
    r = sb("r", [128, n])
    msk = sb("msk", [128, n])
    ab = sb("ab", [128, n])
    sn = sb("sn", [128, n])
    cs = sb("cs", [128, n])
    mg = sb("mg", [128, n])
    hp = sb("hp", [128, 1])
    K = lambda s: tag + s
    S.op("dve", lambda q: q.memset(hp[:], math.pi / 2), writes=[K("hp")])
    S.op("act", lambda q: q.activation(out=dt_[:], in_=a_t[:, 2, :], func=AF.Exp), reads=[K("a")], writes=[K("dt")])
    S.op("dve", lambda q: q.tensor_tensor(out=zre[:], in0=a_t[:, 0, :], in1=dt_[:], op=ALU.mult), reads=[K("a"), K("dt")], writes=[K("zre")])
    S.op("dve", lambda q: q.tensor_tensor(out=zim[:], in0=a_t[:, 1, :], in1=dt_[:], op=ALU.mult), reads=[K("a"), K("dt")], writes=[K("zim")])
    S.op("dve", lambda q: q.memset(E[:, 0, 0, :], 1.0), writes=[K("E0")])
    S.op("dve", lambda q: q.memset(E[:, 1, 0, :], 0.0), writes=[K("E0b")])
    for k in range(1, 9):
        S.op("dve", lambda q: q.tensor_scalar(out=ang[:], in0=zim[:], scalar1=float(k), scalar2=None, op0=ALU.mult), reads=[K("zim")], writes=[K("ang")])
        S.op("dve", lambda q: q.tensor_scalar(out=tmp[:], in0=ang[:], scalar1=1.0 / TWO_PI, scalar2=MAGIC, op0=ALU.mult, op1=ALU.add), reads=[K("ang")], writes=[K("tmp")])
        S.op("dve", lambda q: q.tensor_scalar(out=kf[:], in0=tmp[:], scalar1=MAGIC, scalar2=None, op0=ALU.subtract), reads=[K("tmp")], writes=[K("kf")])
        S.op("dve", lambda q: q.scalar_tensor_tensor(out=r[:], in0=kf[:], scalar=-TWO_PI, in1=ang[:], op0=ALU.mult, op1=ALU.add), reads=[K("kf"), K("ang")], writes=[K("r")])
        S.op("dve", lambda q: q.tensor_single_scalar(out=msk[:], in_=r[:], scalar=math.pi, op=ALU.is_gt), reads=[K("r")], writes=[K("msk")])
        S.op("dve", lambda q: q.scalar_tensor_tensor(out=r[:], in0=msk[:], scalar=-TWO_PI, in1=r[:], op0=ALU.mult, op1=ALU.add), reads=[K("msk"), K("r")], writes=[K("r")])
        S.op("dve", lambda q: q.tensor_single_scalar(out=msk[:], in_=r[:], scalar=-math.pi, op=ALU.is_lt), reads=[K("r")], writes=[K("msk")])
        S.op("dve", lambda q: q.scalar_tensor_tensor(out=r[:], in0=msk[:], scalar=TWO_PI, in1=r[:], op0=ALU.mult, op1=ALU.add), reads=[K("msk"), K("r")], writes=[K("r")])
        S.op("dve", lambda q: q.tensor_scalar(out=r[:], in0=r[:], scalar1=-PI, scalar2=PI, op0=ALU.max, op1=ALU.min), reads=[K("r")], writes=[K("r")])
        S.op("act", lambda q: q.activation(out=ab[:], in_=r[:], func=AF.Abs), reads=[K("r")], writes=[K("ab")])
        S.op("act", lambda q: q.activation(out=sn[:], in_=r[:], func=AF.Sin), reads=[K("r")], writes=[K("sn")])
        S.op("act", lambda q: q.activation(out=cs[:], in_=ab[:], func=AF.Sin, scale=-1.0, bias=hp[:]), reads=[K("ab"), K("hp")], writes=[K("cs")])
        S.op("act", lambda q, k=k: q.activation(out=mg[:], in_=zre[:], func=AF.Exp, scale=float(k)), reads=[K("zre")], writes=[K("mg")])
        S.op("dve", lambda q, k=k: q.tensor_tensor(out=E[:, 0, k, :], in0=mg[:], in1=cs[:], op=ALU.mult), reads=[K("mg"), K("cs")], writes=[K(f"Er{k}")])
        S.op("dve", lambda q, k=k: q.tensor_tensor(out=E[:, 1, k, :], in0=mg[:], in1=sn[:], op=ALU.mult), reads=[K("mg"), K("sn")], writes=[K(f"Ei{k}")])
    return E


def stage_prep(S, nc, dr, A8):
    with contextlib.ExitStack() as es:
        sb = lambda name, shape, dt=F32: es.enter_context(nc.sbuf_tensor(name, shape, dt))
        a_ri = sb("pa_ri", [128, 3, G])
        b_ri = sb("pb_ri", [128, 2, G, H])
        c_ri = sb("pc_ri", [128, G, H])
        a_gp = sb("pa_gp", [128, 3, G // 2])
        c_gp = sb("pc_gp", [128, 2, G // 2, H])
        d_sh = sb("pd_sh", [128, G])
        ident = sb("pident", [128, 128])
        identb = sb("pidentb", [128, 128], BF16)
        ld = [("ria", a_ri, "a_ri"), ("b_ri", b_ri, "b_ri"), ("c_ri", c_ri, "c_ri"), ("gpa", a_gp, "a_gp"),
              ("c_gp", c_gp, "c_gp"), ("d_sh", d_sh, "d_sh"), ("ident", ident, "ident")]
        for i, (key, t, dn) in enumerate(ld):
            S.dma("sp", lambda q, t=t, dn=dn: q.dma_start(out=t[:], in_=dr[dn]), writes=[key], semkey=f"ld{i % 4}")
        S.op("dve", lambda q: q.tensor_copy(out=identb[:], in_=ident[:]), reads=["ident"], writes=["identb"])

        Eri = emit_etables(S, nc, es, a_ri, G, "ri")
        Egp = emit_etables(S, nc, es, a_gp, G // 2, "gp")
        S.op("dve", lambda q: q.tensor_copy(out=A8[:, 0, :], in_=Egp[:, 0, 8, :]), reads=["gpEr8"], writes=["A8r"])
        S.op("dve", lambda q: q.tensor_copy(out=A8[:, 1, :], in_=Egp[:, 1, 8, :]), reads=["gpEi8"], writes=["A8i"])

        with contextlib.ExitStack() as es2:
            sb2 = lambda name, shape, dt=F32: es2.enter_context(nc.sbuf_tensor(name, shape, dt))
            W2 = sb2("pW2", [128, G // 2, 2, 8, H], BF16)
            u1 = sb2("pu1", [128, G // 2, H]); u2 = sb2("pu2", [128, G // 2, H])
            cgr, cgi = c_gp[:, 0], c_gp[:, 1]
            for t in range(8):
                k = t + 1
                S.op("dve", lambda q, k=k: q.tensor_tensor(out=u1[:], in0=cgr, in1=bc_last(Egp[:, 0, k, :], H), op=ALU.mult), reads=["c_gp", f"gpEr{k}"], writes=["u1"])
                S.op("dve", lambda q, k=k: q.tensor_tensor(out=u2[:], in0=cgi, in1=bc_last(Egp[:, 1, k, :], H), op=ALU.mult), reads=["c_gp", f"gpEi{k}"], writes=["u2"])
                S.op("dve", lambda q, t=t: q.tensor_tensor(out=W2[:, :, 0, t, :], in0=u1[:], in1=u2[:], op=ALU.subtract), reads=["u1", "u2"], writes=["W2"])
                S.op("dve", lambda q, k=k: q.tensor_tensor(out=u1[:], in0=cgr, in1=bc_last(Egp[:, 1, k, :], H), op=ALU.mult), reads=["c_gp", f"gpEi{k}", "W2"], writes=["u1"])
                S.op("dve", lambda q, k=k: q.tensor_tensor(out=u2[:], in0=cgi, in1=bc_last(Egp[:, 0, k, :], H), op=ALU.mult), reads=["c_gp", f"gpEr{k}", "W2"], writes=["u2"])
                S.op("dve", lambda q: q.tensor_tensor(out=u1[:], in0=u1[:], in1=u2[:], op=ALU.add), reads=["u1", "u2"], writes=["u1"])
                S.op("dve", lambda q, t=t: q.tensor_scalar(out=W2[:, :, 1, t, :], in0=u1[:], scalar1=-1.0, scalar2=None, op0=ALU.mult), reads=["u1"], writes=["W2"])
            S.dma("sp", lambda q: q.dma_start(out=dr["s5w_2"], in_=W2[:].rearrange("p a b t h -> p a b (t h)")), reads=["W2"], writes=["s5w_2"], semkey="st0")
            S.barrier()
        den = sb("pden", [128, G]); t1 = sb("pt1", [128, G]); t2 = sb("pt2", [128, G])
        nre = sb("pnre", [128, G]); cre = sb("pcre", [128, G]); cim = sb("pcim", [128, G])
        m0 = sb("pm0", [128, 1]); m1 = sb("pm1", [128, 1]); sgn = sb("psgn", [128, 1])
        cA = sb("pcA", [128, G]); cB = sb("pcB", [128, G]); ncA = sb("pncA", [128, G])
        S.op("dve", lambda q: q.memset(m0[0:64, :], 1.0), writes=["m0a"])
        S.op("dve", lambda q: q.memset(m0[64:128, :], 0.0), writes=["m0b"])
        S.op("dve", lambda q: q.memset(m1[0:64, :], 0.0), writes=["m1a"])
        S.op("dve", lambda q: q.memset(m1[64:128, :], 1.0), writes=["m1b"])
        S.op("dve", lambda q: q.memset(sgn[0:64, :], 1.0), writes=["sga"])
        S.op("dve", lambda q: q.memset(sgn[64:128, :], -1.0), writes=["sgb"])
        are, aim = a_ri[:, 0, :], a_ri[:, 1, :]
        S.op("dve", lambda q: q.tensor_tensor(out=den[:], in0=are, in1=are, op=ALU.mult), reads=["ria"], writes=["den"])
        S.op("dve", lambda q: q.tensor_tensor(out=t1[:], in0=aim, in1=aim, op=ALU.mult), reads=["ria"], writes=["t1"])
        S.op("dve", lambda q: q.tensor_tensor(out=den[:], in0=den[:], in1=t1[:], op=ALU.add), reads=["den", "t1"], writes=["den"])
        S.op("dve", lambda q: q.reciprocal(out=den[:], in_=den[:]), reads=["den"], writes=["den"])
        S.op("dve", lambda q: q.tensor_scalar(out=nre[:], in0=Eri[:, 0, 1, :], scalar1=-1.0, scalar2=None, op0=ALU.add), reads=["riEr1"], writes=["nre"])
        nim = Eri[:, 1, 1, :]
        S.op("dve", lambda q: q.tensor_tensor(out=t1[:], in0=nre[:], in1=are, op=ALU.mult), reads=["nre", "ria"], writes=["t1"])
        S.op("dve", lambda q: q.tensor_tensor(out=t2[:], in0=nim, in1=aim, op=ALU.mult), reads=["riEi1", "ria"], writes=["t2"])
        S.op("dve", lambda q: q.tensor_tensor(out=t1[:], in0=t1[:], in1=t2[:], op=ALU.add), reads=["t1", "t2"], writes=["t1"])
        S.op("dve", lambda q: q.tensor_tensor(out=cre[:], in0=t1[:], in1=den[:], op=ALU.mult), reads=["t1", "den"], writes=["cre"])
        S.op("dve", lambda q: q.tensor_tensor(out=t1[:], in0=nim, in1=are, op=ALU.mult), reads=["riEi1", "ria", "cre"], writes=["t1"])
        S.op("dve", lambda q: q.tensor_tensor(out=t2[:], in0=nre[:], in1=aim, op=ALU.mult), reads=["nre", "ria"], writes=["t2"])
        S.op("dve", lambda q: q.tensor_tensor(out=t1[:], in0=t1[:], in1=t2[:], op=ALU.subtract), reads=["t1", "t2"], writes=["t1"])
        S.op("dve", lambda q: q.tensor_tensor(out=cim[:], in0=t1[:], in1=den[:], op=ALU.mult), reads=["t1", "den"], writes=["cim"])
        S.op("dve", lambda q: q.tensor_scalar(out=t1[:], in0=cre[:], scalar1=m0[:], scalar2=None, op0=ALU.mult), reads=["cre", "m0a", "m0b", "cim"], writes=["t1"])
        S.op("dve", lambda q: q.scalar_tensor_tensor(out=cA[:], in0=cim[:], scalar=m1[:], in1=t1[:], op0=ALU.mult, op1=ALU.add), reads=["cim", "m1a", "m1b", "t1"], writes=["cA"])
        S.op("dve", lambda q: q.tensor_scalar(out=t2[:], in0=cre[:], scalar1=m1[:], scalar2=None, op0=ALU.mult), reads=["cre", "m1a", "m1b"], writes=["t2"])
        S.op("dve", lambda q: q.tensor_scalar(out=t1[:], in0=cim[:], scalar1=m0[:], scalar2=-1.0, op0=ALU.mult, op1=ALU.mult), reads=["cim", "m0a", "m0b", "cA"], writes=["t1"])
        S.op("dve", lambda q: q.tensor_tensor(out=cB[:], in0=t1[:], in1=t2[:], op=ALU.add), reads=["t1", "t2"], writes=["cB"])
        S.op("dve", lambda q: q.tensor_scalar(out=ncA[:], in0=cA[:], scalar1=-1.0, scalar2=None, op0=ALU.mult), reads=["cA"], writes=["ncA"])
        P1 = sb("pP1", [128, G, H]); P2 = sb("pP2", [128, G, H]); tq = sb("ptq", [128, G, H])
        bre, bim = b_ri[:, 0], b_ri[:, 1]
        S.op("dve", lambda q: q.tensor_tensor(out=P1[:], in0=bre, in1=bc_last(cA[:], H), op=ALU.mult), reads=["b_ri", "cA"], writes=["P1"])
        S.op("dve", lambda q: q.tensor_tensor(out=tq[:], in0=bim, in1=bc_last(cB[:], H), op=ALU.mult), reads=["b_ri", "cB"], writes=["tq"])
        S.op("dve", lambda q: q.tensor_tensor(out=P1[:], in0=P1[:], in1=tq[:], op=ALU.add), reads=["P1", "tq"], writes=["P1"])
        S.op("dve", lambda q: q.tensor_tensor(out=P2[:], in0=bre, in1=bc_last(cB[:], H), op=ALU.mult), reads=["b_ri", "cB"], writes=["P2"])
        S.op("dve", lambda q: q.tensor_tensor(out=tq[:], in0=bim, in1=bc_last(ncA[:], H), op=ALU.mult), reads=["b_ri", "ncA", "P1"], writes=["tq"])
        S.op("dve", lambda q: q.tensor_tensor(out=P2[:], in0=P2[:], in1=tq[:], op=ALU.add), reads=["P2", "tq"], writes=["P2"])
        Zb = sb("pZb", [128, G, 15, H], BF16)
        S.op("dve", lambda q: q.memset(Zb[:, :, 8:15, :], 0.0), writes=["Zbz"])
        Zt = sb("pZt", [128, G, H])
        for j in range(8):
            k = 7 - j
            S.op("dve", lambda q, k=k: q.tensor_tensor(out=tq[:], in0=P1[:], in1=bc_last(Eri[:, 0, k, :], H), op=ALU.mult),
                 reads=["P1", f"riEr{k}" if k else "riE0"], writes=["tq"])
            S.op("dve", lambda q, k=k: q.tensor_tensor(out=Zt[:], in0=P2[:], in1=bc_last(Eri[:, 1, k, :], H), op=ALU.mult),
                 reads=["P2", f"riEi{k}" if k else "riE0b"], writes=["Zt"])
            S.op("dve", lambda q, j=j: q.tensor_tensor(out=Zb[:, :, j, :], in0=tq[:], in1=Zt[:], op=ALU.add),
                 reads=["tq", "Zt"], writes=[f"Zb{j}"])
        Zkeys = [f"Zb{j}" for j in range(8)] + ["Zbz"]
        Cm = sb("pCm", [128, G, H], BF16)
        S.op("dve", lambda q: q.tensor_scalar(out=Cm[:], in0=c_ri[:], scalar1=sgn[:], scalar2=None, op0=ALU.mult), reads=["c_ri", "sga", "sgb"], writes=["Cm"])

        MW = sb("pMW", [128, 2, GB, 2, 128], BF16)
        ps = es.enter_context(nc.psum_tensor("pps", [128, 4, 512], F32))
        Zf = Zb[:].rearrange("p g j h -> p g (j h)")
        for g in range(G):
            bk = g % 4
            def mm(q, g=g, bk=bk):
                for t in range(8):
                    q.matmul(ps[:, bk, t * 16:(t + 1) * 16], lhsT=Zf[:, g, (7 - t) * 16:(7 - t) * 16 + 128], rhs=Cm[:, g, :], start=True, stop=True)
                return q.matmul(ps[:, bk, 128:256], lhsT=Zf[:, g, 0:128], rhs=identb[:], start=True, stop=True)
            S.op("pe", mm, reads=Zkeys + ["Cm", "identb"], writes=[f"pps{bk}"])
            S.op("dve", lambda q, g=g, bk=bk: q.scalar_tensor_tensor(out=MW[:, (g // GB) % 2, g % GB, 0, :], in0=ident[:], scalar=d_sh[:, g:g + 1], in1=ps[:, bk, 0:128], op0=ALU.mult, op1=ALU.add),
                 reads=["ident", "d_sh"], writes=[f"pps{bk}", f"MW{(g // GB) % 2}"])
            S.op("act", lambda q, g=g, bk=bk: q.activation(out=MW[:, (g // GB) % 2, g % GB, 1, :], in_=ps[:, bk, 128:256], func=AF.Copy),
                 writes=[f"pps{bk}", f"MWb{(g // GB) % 2}"])
            if g % GB == GB - 1:
                b4 = g // GB
                S.dma("sp", lambda q, b4=b4: q.dma_start(out=dr["s5w_m"][:, b4 * GB:(b4 + 1) * GB], in_=MW[:, b4 % 2]),
                      reads=[f"MW{b4 % 2}", f"MWb{b4 % 2}"], writes=[f"s5w_m{b4}"], semkey=f"st{b4 % 2}")
        S.barrier()


class Ctx:
    pass


_UN = [0]


def uname(b):
    _UN[0] += 1
    return f"{b}_{_UN[0]}"


class Arena:
    def __init__(self, nc, es, nbytes):
        self.n = nbytes // 4
        self.t = es.enter_context(nc.sbuf_tensor("arena", [128, self.n], F32))
        self.lo, self.hi = 0, self.n

    def alloc(self, shape, dt=F32, hi=False):
        assert shape[0] == 128
        ne = 1
        for d in shape[1:]:
            ne *= d
        words = ne if dt == F32 else (ne + 1) // 2
        words = (words + 7) // 8 * 8
        if hi:
            self.hi -= words
            off = self.hi
        else:
            off = self.lo
            self.lo += words
        assert self.lo <= self.hi, f"arena overflow lo={self.lo} hi={self.hi} n={self.n}"
        ap = self.t[:, off:off + (ne if dt == F32 else (ne + 1) // 2)]
        if dt != F32:
            ap = ap.bitcast(dt)
            if ne % 2:
                ap = ap[:, 0:ne]
        dims = shape[1:]
        if len(dims) > 1:
            names = "abcdef"[:len(dims)]
            pat = "p (" + " ".join(names) + ") -> p " + " ".join(names)
            ap = ap.rearrange(pat, **{names[i]: dims[i] for i in range(len(dims) - 1)})
        return ap

    def mark(self):
        return (self.lo, self.hi)

    def release(self, m):
        self.lo, self.hi = m

QE = 8192


def wview(C, q0, k, n):
    return C.wring[:, q0 * QE: q0 * QE + k * n].rearrange("p (k n) -> p k n", n=n)


def wkeys(q0, nq):
    return [f"wq{i}" for i in range(q0, q0 + nq)]


def load_w(C, src2d, q0, nq, k, n):
    dst = wview(C, q0, k, n)
    C.S.dma("pool", lambda q: q.dma_start(out=dst, in_=src2d.rearrange("(k p) n -> p k n", p=128)),
            writes=wkeys(q0, nq), semkey=f"wq{q0}")
    return dst


def nquarters(k, n):
    return max(1, -(-(k * n) // QE))


def stat_rstd(C, es, ps2, acc, n, out_t, tag):
    S = C.S
    def mm(q):
        q.matmul(ps2[:, 0:512], lhsT=C.ones[:], rhs=acc[:, 0:512], start=True, stop=True)
        return q.matmul(ps2[:, 512:1024], lhsT=C.ones[:], rhs=acc[:, 512:1024], start=True, stop=True)
    S.op("pe", mm, reads=[tag + "acc", "ones"], writes=[tag + "pstat"])
    S.op("act", lambda q: q.activation(out=out_t[:], in_=ps2[:], func=AF.Sqrt, scale=1.0 / n, bias=C.epsb[:]),
         reads=["epsb"], writes=[tag + "pstat", tag + "rs"])
    S.op("dve", lambda q: q.reciprocal(out=out_t[:], in_=out_t[:]), writes=[tag + "rs"])


def stage_x(C, xT, xn, tag):
    nc, S = C.nc, C.S
    A = C.A
    _m = A.mark()
    with contextlib.ExitStack() as es:
        xc = A.alloc([128, 3, TOK], F32)
        sq = A.alloc([128, 2, TOK], F32)
        rstd = A.alloc([128, TOK], F32)
        ps = es.enter_context(nc.psum_tensor(uname(tag + "psx"), [128, 2, 512], F32))
        xTv = xT.rearrange("(k p) t -> p k t", p=128)
        for k in range(KC):
            r = k % 3
            S.dma("sp", lambda q: q.dma_start(out=xc[:, r], in_=xTv[:, k]), writes=[f"xc{r}"], semkey=f"xc{r}")
            S.op("act", lambda q: q.activation(out=sq[:, k % 2], in_=xc[:, r], func=AF.Square), reads=[f"xc{r}"], writes=[f"sq{k % 2}"])
            def mm(q):
                q.matmul(ps[:, 0], lhsT=C.ones[:], rhs=sq[:, k % 2, 0:512], start=(k == 0), stop=(k == KC - 1))
                return q.matmul(ps[:, 1], lhsT=C.ones[:], rhs=sq[:, k % 2, 512:1024], start=(k == 0), stop=(k == KC - 1))
            S.op("pe", mm, reads=[f"sq{k % 2}", "ones"], writes=["psx"])
        S.op("act", lambda q: q.activation(out=rstd[:], in_=ps[:].rearrange("p a b -> p (a b)"), func=AF.Sqrt, scale=1.0 / D, bias=C.epsb[:]),
             reads=["epsb"], writes=["psx", "rstd"])
        S.op("dve", lambda q: q.reciprocal(out=rstd[:], in_=rstd[:]), writes=["rstd"])
        for k in range(KC):
            r = k % 3
            S.dma("sp", lambda q: q.dma_start(out=xc[:, r], in_=xTv[:, k]), writes=[f"xc{r}"], semkey=f"xc{r}")
            S.op("dve", lambda q: q.scalar_tensor_tensor(out=xn[:, k, :], in0=xc[:, r], scalar=C.gpre[:, k:k + 1], in1=rstd[:], op0=ALU.mult, op1=ALU.mult),
                 reads=[f"xc{r}", "rstd", "gpre"], writes=["xn"])
        S.barrier()
    A.release(_m)


def u_proj(C, xn, Utok):
    nc, S = C.nc, C.S
    if True:
        with contextlib.ExitStack() as es1:
            ps = es1.enter_context(nc.psum_tensor(uname("psu"), [128, 4, 512], F32))
            CW = min(512, DS)
            nq = nquarters(KC, CW)
            nslot = 4 // nq
            for j in range(DS // CW):
                q0 = (j % nslot) * nq
                w = load_w(C, C.dr["w_in"][:, j * CW:(j + 1) * CW], q0, nq, KC, CW)
                for s in range(8):
                    b = (j * 8 + s) % 4
                    def mm(q):
                        for k in range(KC):
                            ins = q.matmul(ps[:, b, 0:CW], lhsT=xn[:, k, s * 128:(s + 1) * 128], rhs=w[:, k, :], start=(k == 0), stop=(k == KC - 1))
                        return ins
                    S.op("pe", mm, reads=wkeys(q0, nq) + ["xn"], writes=[f"psu{b}"])
                    if s % 2 == 0:
                        S.op("act", lambda q: q.activation(out=Utok[:, j * CW // 16:(j + 1) * CW // 16, s, :], in_=ps[:, b, 0:CW].rearrange("p (g h) -> p g h", h=16), func=AF.Copy), writes=[f"psu{b}", f"Utok{s}"])
                    else:
                        S.op("dve", lambda q: q.tensor_copy(out=Utok[:, j * CW // 16:(j + 1) * CW // 16, s, :], in_=ps[:, b, 0:CW].rearrange("p (g h) -> p g h", h=16)), writes=[f"psu{b}", f"Utok{s}"])
        S.barrier()


def u_states(C, Utok, Ug, Xs):
    nc, S = C.nc, C.S
    A = C.A
    _m = A.mark()
    if True:
        ukeys = [f"Utok{s}" for s in range(8)]
        with contextlib.ExitStack() as es2:
            pst = es2.enter_context(nc.psum_tensor(uname("pst"), [128, 4, 1024], BF16))
            px = es2.enter_context(nc.psum_tensor(uname("px"), [128, 2, 2, 512], F32))
            wS = C.wring[:, 0:4096].rearrange("p (a b c d) -> p a b c d", a=2, b=8, c=2)
            for blk in range(G // 8):
                sl = blk % 2
                S.dma("sp", lambda q: q.dma_start(out=wS[:, sl], in_=C.dr["s5w_m"][:, blk * 8:(blk + 1) * 8]), writes=[f"wS{sl}"], semkey=f"wS{sl}")
                for gi in range(8):
                    g = blk * 8 + gi
                    b = g % 4
                    S.op("pe", lambda q: q.transpose(out=pst[:, b, 0:128], in_=Utok[:, g].rearrange("p s h -> p (s h)"), identity=C.identb[:]),
                         reads=ukeys + ["identb"], writes=[f"pst{b}"])
                    if g % 2 == 0:
                        S.op("act", lambda q: q.activation(out=Ug[:, g, :], in_=pst[:, b, 0:128], func=AF.Copy), writes=[f"pst{b}", f"Ug{g % 2}"])
                    else:
                        S.op("dve", lambda q: q.tensor_copy(out=Ug[:, g, :], in_=pst[:, b, 0:128]), writes=[f"pst{b}", f"Ug{g % 2}"])
                def mmx(q):
                    for gi in range(8):
                        g = blk * 8 + gi
                        hb = (g % 2) * 64
                        col = (gi // 2) * 128
                        for ri in range(2):
                            ins = q.matmul(px[hb:hb + 64, sl, ri, col:col + 128], lhsT=wS[:, sl, gi, 1, ri * 64:(ri + 1) * 64], rhs=Ug[:, g, :], start=True, stop=True)
                    return ins
                S.op("pe", mmx, reads=[f"wS{sl}", "Ug0", "Ug1"], writes=[f"px{sl}"])
                S.op("act", lambda q: q.activation(out=Xs[:, 0, blk * 4:(blk + 1) * 4, :], in_=px[:, sl, 0].rearrange("p (a c) -> p a c", c=128), func=AF.Copy),
                     writes=[f"px{sl}", "Xs"])
                S.op("dve", lambda q: q.tensor_copy(out=Xs[:, 1, blk * 4:(blk + 1) * 4, :], in_=px[:, sl, 1].rearrange("p (a c) -> p a c", c=128)),
                     writes=[f"px{sl}", "Xs"])
        S.barrier()
    A.release(_m)


def s5_scan(C, Xs):
    nc, S = C.nc, C.S
    A = C.A
    _m = A.mark()
    GP = G // 2
    with contextlib.ExitStack() as es:
        t = [A.alloc([128, GP], F32) for i in range(4)]
        ar, ai = C.A8[:, 0, :], C.A8[:, 1, :]
        for c in range(TOK // 8):
            pr = C.Sinit[:, 0, :] if c == 0 else Xs[:, 0, :, c - 1]
            pi = C.Sinit[:, 1, :] if c == 0 else Xs[:, 1, :, c - 1]
            cr, ci = Xs[:, 0, :, c], Xs[:, 1, :, c]
            S.op("dve", lambda q: q.tensor_tensor(out=t[0][:], in0=pr, in1=ar, op=ALU.mult), reads=["Xs", "Sinit"], writes=["t0"])
            S.op("dve", lambda q: q.tensor_tensor(out=t[1][:], in0=pi, in1=ai, op=ALU.mult), reads=["Xs", "Sinit"], writes=["t1"])
            S.op("dve", lambda q: q.tensor_tensor(out=t[2][:], in0=pi, in1=ar, op=ALU.mult), reads=["Xs", "Sinit"], writes=["t2"])
            S.op("dve", lambda q: q.tensor_tensor(out=t[3][:], in0=pr, in1=ai, op=ALU.mult), reads=["Xs", "Sinit"], writes=["t3"])
            S.op("dve", lambda q: q.tensor_tensor(out=t[0][:], in0=t[0][:], in1=t[1][:], op=ALU.subtract), reads=["t0", "t1"], writes=["t0"])
            S.op("dve", lambda q: q.tensor_tensor(out=t[2][:], in0=t[2][:], in1=t[3][:], op=ALU.add), reads=["t2", "t3"], writes=["t2"])
            S.op("dve", lambda q: q.tensor_tensor(out=cr, in0=cr, in1=t[0][:], op=ALU.add), reads=["t0"], writes=["Xs"])
            S.op("dve", lambda q: q.tensor_tensor(out=ci, in0=ci, in1=t[2][:], op=ALU.add), reads=["t2"], writes=["Xs"])
        S.barrier()
    A.release(_m)


def stage_y(C, Ug, Sp):
    nc, S = C.nc, C.S
    A = C.A
    _m = A.mark()
    with contextlib.ExitStack() as es:
        pY = es.enter_context(nc.psum_tensor(uname("pY"), [128, 2, 1024], F32))
        pT = es.enter_context(nc.psum_tensor(uname("pT"), [128, 2, 1024], F32))
        wS = C.wring[:, 0:4096].rearrange("p (a b c d) -> p a b c d", a=2, b=8, c=2)
        w2 = C.wring[:, 4096:6144].rearrange("p (a b c d) -> p a b c d", a=2, b=4, c=2)
        Yb = A.alloc([128, 2, 1024], F32)
        ga = A.alloc([128, 2, 1024], F32)
        ybc = A.alloc([128, 2, TOK], BF16)
        for blk in range(G // 8):
            sl = blk % 2
            S.dma("sp", lambda q: q.dma_start(out=wS[:, sl], in_=C.dr["s5w_m"][:, blk * 8:(blk + 1) * 8]), writes=[f"wSy{sl}"], semkey=f"wS{sl}")
            S.dma("sp", lambda q: q.dma_start(out=w2[:, sl], in_=C.dr["s5w_2"][:, blk * 4:(blk + 1) * 4].rearrange("p a (b t) -> p a b t", b=2)), writes=[f"w2y{sl}"], semkey=f"w2{sl}")
            def mm(q):
                for gi in range(8):
                    g = blk * 8 + gi
                    hb = (g % 2) * 64
                    o = pY[:, sl, gi * 128:(gi + 1) * 128]
                    q.matmul(o, lhsT=Ug[:, g, :], rhs=wS[:, sl, gi, 0, :], start=True, stop=False)
                    q.matmul(o, lhsT=Sp[hb:hb + 64, 0, g // 2, :], rhs=w2[hb:hb + 64, sl, gi // 2, 0, :], start=False, stop=False)
                    ins = q.matmul(o, lhsT=Sp[hb:hb + 64, 1, g // 2, :], rhs=w2[hb:hb + 64, sl, gi // 2, 1, :], start=False, stop=True)
                return ins
            S.op("pe", mm, reads=[f"wSy{sl}", f"w2y{sl}", "Ug", "Sp"], writes=[f"pY{sl}"])
            S.op("act", lambda q: q.activation(out=Yb[:, sl].rearrange("c (t g h) -> c g t h", t=8, g=8), in_=pY[:, sl].rearrange("c (g t h) -> c g t h", g=8, t=8), func=AF.Copy), writes=[f"pY{sl}", f"Yb{sl}"])
            def tr(q):
                for t in range(8):
                    ins = q.transpose(out=pT[:, sl, t * 128:(t + 1) * 128], in_=Yb[:, sl, t * 128:(t + 1) * 128], identity=C.ident[:])
                return ins
            S.op("pe", tr, reads=[f"Yb{sl}", "ident"], writes=[f"pT{sl}"])
            S.op("act", lambda q: q.activation(out=ga[:, sl], in_=pT[:, sl], func=AF.Square), writes=[f"pT{sl}", f"ga{sl}"])
            S.op("dve", lambda q: q.tensor_scalar(out=ga[:, sl], in0=ga[:, sl], scalar1=0.044715, scalar2=1.0, op0=ALU.mult, op1=ALU.add), writes=[f"ga{sl}"])
            S.op("dve", lambda q: q.tensor_tensor(out=ga[:, sl], in0=ga[:, sl], in1=pT[:, sl], op=ALU.mult), writes=[f"pT{sl}", f"ga{sl}"])
            S.op("act", lambda q: q.activation(out=ga[:, sl], in_=ga[:, sl], func=AF.Sigmoid, scale=1.5957691216), writes=[f"ga{sl}"])
            S.op("dve", lambda q: q.tensor_tensor(out=ybc[:, sl], in0=ga[:, sl], in1=pT[:, sl], op=ALU.mult), writes=[f"pT{sl}", f"ga{sl}", f"ybc{sl}"])
            S.dma("sp", lambda q: q.dma_start(out=C.dr["yb_scr"][:, blk, :], in_=ybc[:, sl]), reads=[f"ybc{sl}"], writes=["yb_scr"], semkey=f"ybst{sl}")
        S.barrier()
    A.release(_m)


def stage_g(C):
    nc, S = C.nc, C.S
    A = C.A
    _m = A.mark()
    with contextlib.ExitStack() as es:
        yb = A.alloc([128, NSC, TOK], BF16)
        S.dma("sp", lambda q: q.dma_start(out=yb, in_=C.dr["yb_scr"]), writes=["yb"], semkey="ybld")
        mxc = A.alloc([128, 2, TOK], BF16)
        y2 = A.alloc([128, NSC, TOK], F32)
        acc = A.alloc([128, TOK], F32)
        tmp = A.alloc([128, TOK], F32)
        sig = A.alloc([128, 2, TOK], F32)
        rs = A.alloc([128, TOK], F32)
        pz = es.enter_context(nc.psum_tensor(uname("pz"), [128, 3, 1024], F32))
        pstat = es.enter_context(nc.psum_tensor(uname("gpstat"), [128, 1024], F32))
        CW = min(512, DS)
        nq = nquarters(NSC, CW)
        nslot = 4 // nq
        for j in range(DS // CW):
            q0 = (j % nslot) * nq
            w = load_w(C, C.dr["w_glu"][:, j * CW:(j + 1) * CW], q0, nq, NSC, CW)
            for fi in range(CW // 128):
                f = j * (CW // 128) + fi
                b = f % 3
                def mm(q):
                    for hf in range(2):
                        for k in range(NSC):
                            ins = q.matmul(pz[:, b, hf * 512:(hf + 1) * 512], lhsT=w[:, k, fi * 128:(fi + 1) * 128], rhs=yb[:, k, hf * 512:(hf + 1) * 512], start=(k == 0), stop=(k == NSC - 1))
                    return ins
                S.op("pe", mm, reads=wkeys(q0, nq) + ["yb"], writes=[f"pz{b}"])
                S.op("act", lambda q: q.activation(out=sig[:, f % 2], in_=pz[:, b], func=AF.Sigmoid, bias=C.bglu[:, f:f + 1]), reads=["vec"], writes=[f"pz{b}", f"gsig{f % 2}"])
                S.op("dve", lambda q: q.tensor_tensor(out=y2[:, f, :], in0=yb[:, f, :], in1=sig[:, f % 2], op=ALU.mult), reads=["yb", f"gsig{f % 2}"], writes=["y2"])
                if f == 0:
                    S.op("dve", lambda q: q.tensor_tensor(out=acc[:], in0=y2[:, f, :], in1=y2[:, f, :], op=ALU.mult), reads=["y2"], writes=["gacc"])
                else:
                    S.op("dve", lambda q: q.tensor_tensor(out=tmp[:], in0=y2[:, f, :], in1=y2[:, f, :], op=ALU.mult), reads=["y2"], writes=["gtmp"])
                    S.op("dve", lambda q: q.tensor_tensor(out=acc[:], in0=acc[:], in1=tmp[:], op=ALU.add), reads=["gtmp"], writes=["gacc"])
        stat_rstd(C, es, pstat, acc, DS, rs, "g")
        for f in range(NSC):
            S.op("dve", lambda q: q.scalar_tensor_tensor(out=mxc[:, f % 2], in0=y2[:, f, :], scalar=C.gs[:, f:f + 1], in1=rs[:], op0=ALU.mult, op1=ALU.mult),
                 reads=["y2", "grs", "vec"], writes=[f"mxc{f % 2}"])
            S.dma("sp", lambda q: q.dma_start(out=C.dr["mix_scr"][:, f, :], in_=mxc[:, f % 2]), reads=[f"mxc{f % 2}"], writes=["mix_scr"], semkey=f"mxst{f % 2}")
        S.barrier()
    A.release(_m)


def stage_c(C, xn):
    nc, S = C.nc, C.S
    A = C.A
    _m = A.mark()
    with contextlib.ExitStack() as es:
        acc1 = A.alloc([128, TOK], F32)
        acc2 = A.alloc([128, TOK], F32)
        tmp = A.alloc([128, TOK], F32)
        cv = A.alloc([128, 2, TOK], F32)
        mu = A.alloc([128, TOK], F32)
        rs = A.alloc([128, TOK], F32)
        pstat = es.enter_context(nc.psum_tensor(uname("cpstat"), [128, 1024], F32))
        with contextlib.ExitStack() as es1:
            _m1 = A.mark()
            Hc = A.alloc([128, 2, 8, 132], F32)
            sg = A.alloc([128, 1056], F32)
            pv = es1.enter_context(nc.psum_tensor(uname("pv"), [128, 3, 512], F32))
            pg = es1.enter_context(nc.psum_tensor(uname("pg"), [128, 3, 512], F32))
            CW = min(256, DC)
            FPB = CW // 128
            nq = nquarters(KC, CW)
            assert nq == 1
            pvf = pv[:].rearrange("p a b -> p (a b)")
            pgf = pg[:].rearrange("p a b -> p (a b)")
            for j in range(DC // CW):
                qv, qg = 2 * (j % 2), 2 * (j % 2) + 1
                wv = load_w(C, C.dr["w_in"][:, DS + j * CW: DS + (j + 1) * CW], qv, 1, KC, CW)
                wg = load_w(C, C.dr["w_in"][:, DS + DC + j * CW: DS + DC + (j + 1) * CW], qg, 1, KC, CW)
                for fi in range(FPB):
                    ch = j * FPB + fi
                    hs = ch % 2
                    def mk_mm(pp, w):
                        def mm(q):
                            for (o, r) in ((pp[:, 0], lambda k: xn[:, k, 0:512]), (pp[:, 1], lambda k: xn[:, k, 512:1024]), (pp[:, 2, 0:32], lambda k: C.xh[:, k, :])):
                                for k in range(KC):
                                    ins = q.matmul(o, lhsT=w[:, k, fi * 128:(fi + 1) * 128], rhs=r(k), start=(k == 0), stop=(k == KC - 1))
                            return ins
                        return mm
                    S.op("pe", mk_mm(pv, wv), reads=[f"wq{qv}", "xn", "xh"], writes=["pv"])
                    S.op("pe", mk_mm(pg, wg), reads=[f"wq{qg}", "xn", "xh"], writes=["pg"])
                    S.op("act", lambda q: q.activation(out=sg[:], in_=pgf[:, 0:1056], func=AF.Sigmoid), writes=["pg", "csg"])
                    S.op("dve", lambda q: q.tensor_tensor(out=Hc[:, hs, :, 4:132], in0=sg[:, 0:1024].rearrange("p (s c) -> p s c", c=128),
                                                          in1=pvf[:, 0:1024].rearrange("p (s c) -> p s c", c=128), op=ALU.mult), reads=["csg"], writes=["pv", f"Hc{hs}"])
                    S.op("dve", lambda q: q.tensor_tensor(out=Hc[:, hs, :, 0:4], in0=sg[:, 1024:1056].rearrange("p (s c) -> p s c", c=4),
                                                          in1=pvf[:, 1024:1056].rearrange("p (s c) -> p s c", c=4), op=ALU.mult), reads=["csg"], writes=["pv", f"Hc{hs}"])
                    cvv = cv[:, hs].rearrange("p (s c) -> p s c", c=128)
                    first = True
                    for m in range(31):
                        kk = 30 - m
                        qd, r = m // 8, m % 8
                        wk = C.wdw[:, ch, kk:kk + 1]
                        if first:
                            S.op("dve", lambda q: q.tensor_scalar(out=cvv, in0=Hc[:, hs, :, 4:132], scalar1=wk, scalar2=C.bdw[:, ch:ch + 1], op0=ALU.mult, op1=ALU.add),
                                 reads=[f"Hc{hs}", "vec"], writes=[f"ccv{hs}"])
                            first = False
                            continue
                        if r < 8:
                            S.op("dve", lambda q: q.scalar_tensor_tensor(out=cvv[:, r:8, :], in0=Hc[:, hs, 0:8 - r, 4 - qd:4 - qd + 128], scalar=wk, in1=cvv[:, r:8, :], op0=ALU.mult, op1=ALU.add),
                                 reads=[f"Hc{hs}", "vec"], writes=[f"ccv{hs}"])
                        if r > 0:
                            S.op("dve", lambda q: q.scalar_tensor_tensor(out=cvv[:, 0:r, :], in0=Hc[:, hs, 8 - r:8, 3 - qd:3 - qd + 128], scalar=wk, in1=cvv[:, 0:r, :], op0=ALU.mult, op1=ALU.add),
                                 reads=[f"Hc{hs}", "vec"], writes=[f"ccv{hs}"])
                    if ch == 0:
                        S.op("dve", lambda q: q.tensor_copy(out=acc1[:], in_=cv[:, hs]), reads=[f"ccv{hs}"], writes=["cacc1"])
                        S.op("dve", lambda q: q.tensor_tensor(out=acc2[:], in0=cv[:, hs], in1=cv[:, hs], op=ALU.mult), reads=[f"ccv{hs}"], writes=["cacc2"])
                    else:
                        S.op("dve", lambda q: q.tensor_tensor(out=acc1[:], in0=acc1[:], in1=cv[:, hs], op=ALU.add), reads=[f"ccv{hs}"], writes=["cacc1"])
                        S.op("dve", lambda q: q.tensor_tensor(out=tmp[:], in0=cv[:, hs], in1=cv[:, hs], op=ALU.mult), reads=[f"ccv{hs}"], writes=["ctmp"])
                        S.op("dve", lambda q: q.tensor_tensor(out=acc2[:], in0=acc2[:], in1=tmp[:], op=ALU.add), reads=["ctmp"], writes=["cacc2"])
                    S.dma("sp", lambda q: q.dma_start(out=C.dr["convT"][ch * 128:(ch + 1) * 128, :], in_=cv[:, hs]), reads=[f"ccv{hs}"], writes=[f"convT{ch}"], semkey=f"cvst{hs}")
        A.release(_m1)
        mixed = A.alloc([128, NCC, TOK], BF16)
        def mm1(q):
            q.matmul(pstat[:, 0:512], lhsT=C.ones[:], rhs=acc1[:, 0:512], start=True, stop=True)
            return q.matmul(pstat[:, 512:1024], lhsT=C.ones[:], rhs=acc1[:, 512:1024], start=True, stop=True)
        S.op("pe", mm1, reads=["cacc1", "ones"], writes=["cpstat"])
        S.op("act", lambda q: q.activation(out=mu[:], in_=pstat[:], func=AF.Copy, scale=1.0 / DC), writes=["cpstat", "cmu"])
        S.op("dve", lambda q: q.tensor_tensor(out=tmp[:], in0=mu[:], in1=mu[:], op=ALU.mult), reads=["cmu"], writes=["ctmp"])
        def mm2(q):
            q.matmul(pstat[:, 0:512], lhsT=C.ones[:], rhs=acc2[:, 0:512], start=True, stop=True)
            return q.matmul(pstat[:, 512:1024], lhsT=C.ones[:], rhs=acc2[:, 512:1024], start=True, stop=True)
        S.op("pe", mm2, reads=["cacc2", "ones"], writes=["cpstat"])
        S.op("dve", lambda q: q.scalar_tensor_tensor(out=rs[:], in0=pstat[:], scalar=1.0 / DC, in1=tmp[:], op0=ALU.mult, op1=ALU.subtract), reads=["ctmp"], writes=["cpstat", "crs"])
        S.op("act", lambda q: q.activation(out=rs[:], in_=rs[:], func=AF.Sqrt, bias=C.epsb[:]), reads=["epsb"], writes=["crs"])
        S.op("dve", lambda q: q.reciprocal(out=rs[:], in_=rs[:]), writes=["crs"])
        for ch in range(NCC):
            hs = ch % 2
            S.dma("sp", lambda q: q.dma_start(out=cv[:, hs], in_=C.dr["convT"][ch * 128:(ch + 1) * 128, :]), reads=[f"convT{ch}"], writes=[f"ccv{hs}"], semkey=f"cvld{hs}")
            S.op("dve", lambda q: q.tensor_tensor(out=cv[:, hs], in0=cv[:, hs], in1=mu[:], op=ALU.subtract), reads=["cmu"], writes=[f"ccv{hs}"])
            S.op("dve", lambda q: q.tensor_tensor(out=cv[:, hs], in0=cv[:, hs], in1=rs[:], op=ALU.mult), reads=["crs"], writes=[f"ccv{hs}"])
            S.op("act", lambda q: q.activation(out=cv[:, hs], in_=cv[:, hs], func=AF.Silu, scale=C.lng[:, ch:ch + 1], bias=C.lnb[:, ch:ch + 1]), reads=["vec"], writes=[f"ccv{hs}"])
            if ch == 0:
                S.op("dve", lambda q: q.tensor_tensor(out=acc1[:], in0=cv[:, hs], in1=cv[:, hs], op=ALU.mult), reads=[f"ccv{hs}"], writes=["cacc1"])
            else:
                S.op("dve", lambda q: q.tensor_tensor(out=tmp[:], in0=cv[:, hs], in1=cv[:, hs], op=ALU.mult), reads=[f"ccv{hs}"], writes=["ctmp"])
                S.op("dve", lambda q: q.tensor_tensor(out=acc1[:], in0=acc1[:], in1=tmp[:], op=ALU.add), reads=["ctmp"], writes=["cacc1"])
            S.op("act", lambda q: q.activation(out=mixed[:, ch, :], in_=cv[:, hs], func=AF.Copy), reads=[f"ccv{hs}"], writes=["mixedc"])
        def mm3(q):
            q.matmul(pstat[:, 0:512], lhsT=C.ones[:], rhs=acc1[:, 0:512], start=True, stop=True)
            return q.matmul(pstat[:, 512:1024], lhsT=C.ones[:], rhs=acc1[:, 512:1024], start=True, stop=True)
        S.op("pe", mm3, reads=["cacc1", "ones"], writes=["cpstat"])
        S.op("act", lambda q: q.activation(out=rs[:], in_=pstat[:], func=AF.Sqrt, scale=1.0 / DC, bias=C.epsb[:]), reads=["epsb"], writes=["cpstat", "crs"])
        S.op("dve", lambda q: q.reciprocal(out=rs[:], in_=rs[:]), writes=["crs"])
        for ch in range(NCC):
            S.op("dve", lambda q: q.scalar_tensor_tensor(out=mixed[:, ch, :], in0=mixed[:, ch, :], scalar=C.gc[:, ch:ch + 1], in1=rs[:], op0=ALU.mult, op1=ALU.mult),
                 reads=["crs", "vec"], writes=["mixedc"])
        S.dma("sp", lambda q: q.dma_start(out=C.dr["mix_scr"][:, NSC:NSC + NCC, :], in_=mixed), reads=["mixedc"], writes=["mix_scr"], semkey="mxst0")
        S.barrier()
    A.release(_m)


def stage_o1(C, rstd1):
    nc, S = C.nc, C.S
    A = C.A
    _m = A.mark()
    with contextlib.ExitStack() as es:
        CW = min(512, D)
        NB = D // CW
        ps = es.enter_context(nc.psum_tensor(uname("pso"), [128, 4, 512], F32))
        mixed = A.alloc([128, KC, TOK], BF16)
        hk = KC // 2
        S.dma("sp", lambda q: q.dma_start(out=mixed[:, 0:hk], in_=C.dr["mix_scr"][:, 0:hk]), writes=["mixed"], semkey="mxld0")
        S.dma("sp", lambda q: q.dma_start(out=mixed[:, hk:KC], in_=C.dr["mix_scr"][:, hk:KC]), writes=["mixedc"], semkey="mxld1")
        mt = A.alloc([128, 3, CW], F32)
        junk = A.alloc([128, CW], F32)
        ssq = A.alloc([128, 8, NB], F32)
        nq = nquarters(KC, CW)
        nslot = 4 // nq
        n = 0
        for j in range(NB):
            q0 = (j % nslot) * nq
            w = load_w(C, C.dr["w_out"][:, j * CW:(j + 1) * CW], q0, nq, KC, CW)
            for s in range(8):
                b = n % 4
                r = n % 3
                n += 1
                def mm(q):
                    for k in range(KC):
                        ins = q.matmul(ps[:, b, 0:CW], lhsT=mixed[:, k, s * 128:(s + 1) * 128], rhs=w[:, k, :], start=(k == 0), stop=(k == KC - 1))
                    return ins
                S.op("pe", mm, reads=wkeys(q0, nq) + ["mixed", "mixedc"], writes=[f"pso{b}"])
                S.op("dve", lambda q: q.tensor_copy(out=mt[:, r], in_=ps[:, b, 0:CW]), writes=[f"pso{b}", f"omt{r}"])
                S.op("act", lambda q: q.activation(out=junk[:], in_=ps[:, b, 0:CW], func=AF.Square, accum_out=ssq[:, s, j:j + 1]), writes=[f"pso{b}", "ojunk", "ossq"])
                S.dma("sp", lambda q: q.dma_start(out=C.dr["m_scr"][s * 128:(s + 1) * 128, j * CW:(j + 1) * CW], in_=mt[:, r]), reads=[f"omt{r}"], writes=[f"mscr{s}"], semkey=f"mst{r}")
        for j in range(1, NB):
            S.op("dve", lambda q: q.tensor_tensor(out=ssq[:, :, 0], in0=ssq[:, :, 0], in1=ssq[:, :, j], op=ALU.add), reads=["ossq"], writes=["ossq"])
        S.op("act", lambda q: q.activation(out=rstd1[:], in_=ssq[:, :, 0], func=AF.Sqrt, scale=1.0 / D, bias=C.epsb[:]), reads=["ossq", "epsb"], writes=["rstd1"])
        S.op("dve", lambda q: q.reciprocal(out=rstd1[:], in_=rstd1[:]), writes=["rstd1"])
        S.barrier()
    A.release(_m)


def stage_o2(C, tile, rstd1, h2T):
    nc, S = C.nc, C.S
    A = C.A
    _m = A.mark()
    xnat = C.dr["x_nat"].rearrange("(c s) d -> s c d", s=8)
    outv = C.dr["out"].rearrange("(c s) d -> s c d", s=8)
    with contextlib.ExitStack() as es:
        mrow = A.alloc([128, D], F32)
        xrow = A.alloc([128, D], F32)
        gpm = A.alloc([128, D], F32)
        h2 = A.alloc([128, D], BF16)
        ss2 = A.alloc([128, 1], F32)
        pstT = es.enter_context(nc.psum_tensor(uname("pstT"), [128, 4, 1024], BF16))
        S.dma("sp", lambda q: q.dma_start(out=gpm[:], in_=C.dr["gpm_b"]), writes=["gpm"], semkey="gpm")
        for tb in range(4):
            s = tile * 4 + tb
            S.dma("sp", lambda q: q.dma_start(out=mrow[:], in_=C.dr["m_scr"][s * 128:(s + 1) * 128, :]), reads=[f"mscr{s}"], writes=["mrow"], semkey="mrow")
            S.dma("sp", lambda q: q.dma_start(out=xrow[:], in_=xnat[s]), writes=["xrow"], semkey="xrow")
            S.op("dve", lambda q: q.scalar_tensor_tensor(out=mrow[:], in0=mrow[:], scalar=rstd1[:, s:s + 1], in1=gpm[:], op0=ALU.mult, op1=ALU.mult), reads=["gpm", "rstd1"], writes=["mrow"])
            S.op("dve", lambda q: q.tensor_tensor(out=mrow[:], in0=mrow[:], in1=xrow[:], op=ALU.add), reads=["xrow"], writes=["mrow"])
            S.dma("sp", lambda q: q.dma_start(out=outv[s], in_=mrow[:]), reads=["mrow"], writes=[f"out{s}"], semkey="x1st")
            S.op("act", lambda q: q.activation(out=xrow[:], in_=mrow[:], func=AF.Square, accum_out=ss2[:]), reads=["mrow"], writes=["xrow", "ss2"])
            S.op("act", lambda q: q.activation(out=ss2[:], in_=ss2[:], func=AF.Sqrt, scale=1.0 / D, bias=C.epsb[:]), reads=["epsb"], writes=["ss2"])
            S.op("dve", lambda q: q.reciprocal(out=ss2[:], in_=ss2[:]), writes=["ss2"])
            S.op("act", lambda q: q.activation(out=h2[:], in_=mrow[:], func=AF.Copy, scale=ss2[:]), reads=["mrow", "ss2"], writes=["h2"])
            for k in range(KC):
                b = k % 4
                S.op("pe", lambda q: q.transpose(out=pstT[:, b, 0:128], in_=h2[:, k * 128:(k + 1) * 128], identity=C.identb[:]), reads=["h2", "identb"], writes=[f"pstT{b}"])
                if k % 2 == 0:
                    S.op("act", lambda q: q.activation(out=h2T[:, k, tb * 128:(tb + 1) * 128], in_=pstT[:, b, 0:128], func=AF.Copy, scale=C.g2[:, k:k + 1]), reads=["vec"], writes=[f"pstT{b}", f"h2T{k % 2}"])
                else:
                    S.op("dve", lambda q: q.tensor_scalar(out=h2T[:, k, tb * 128:(tb + 1) * 128], in0=pstT[:, b, 0:128], scalar1=C.g2[:, k:k + 1], scalar2=None, op0=ALU.mult), reads=["vec"], writes=[f"pstT{b}", f"h2T{k % 2}"])
        S.barrier()
    A.release(_m)


def stage_ffn(C, tile, h2T):
    nc, S = C.nc, C.S
    A = C.A
    _m = A.mark()
    outv = C.dr["out"].rearrange("(c s) d -> s c d", s=8)
    TT = 512
    with contextlib.ExitStack() as es:
        _m0 = A.mark()
        act2 = A.alloc([128, FC, TT], BF16)
        rstd3 = C.rstd3p
        with contextlib.ExitStack() as es1:
            pg = es1.enter_context(nc.psum_tensor(uname("fpg"), [128, 3, 512], F32))
            pu = es1.enter_context(nc.psum_tensor(uname("fpu"), [128, 3, 512], F32))
            sgt = A.alloc([128, 2, TT], F32)
            CW = min(256, DFF)
            FPB = CW // 128
            assert DFF % CW == 0 and nquarters(KC, CW) == 1
            for j in range(DFF // CW):
                qg, qu = 2 * (j % 2), 2 * (j % 2) + 1
                wg = load_w(C, C.dr["w_gate"][:, j * CW:(j + 1) * CW], qg, 1, KC, CW)
                wu = load_w(C, C.dr["w_up"][:, j * CW:(j + 1) * CW], qu, 1, KC, CW)
                for fi in range(FPB):
                    f = j * FPB + fi
                    b = f % 3
                    def mk_mm(pp, w):
                        def mm(q):
                            for k in range(KC):
                                ins = q.matmul(pp[:, b], lhsT=w[:, k, fi * 128:(fi + 1) * 128], rhs=h2T[:, k, :], start=(k == 0), stop=(k == KC - 1))
                            return ins
                        return mm
                    S.op("pe", mk_mm(pg, wg), reads=[f"wq{qg}", "h2T0", "h2T1"], writes=[f"fpg{b}"])
                    S.op("pe", mk_mm(pu, wu), reads=[f"wq{qu}", "h2T0", "h2T1"], writes=[f"fpu{b}"])
                    S.op("act", lambda q: q.activation(out=sgt[:, f % 2], in_=pg[:, b], func=AF.Silu), writes=[f"fpg{b}", f"fsg{f % 2}"])
                    S.op("dve", lambda q: q.tensor_tensor(out=act2[:, f, :], in0=sgt[:, f % 2], in1=pu[:, b], op=ALU.mult), reads=[f"fsg{f % 2}"], writes=[f"fpu{b}", "act2"])
            S.barrier()
        with contextlib.ExitStack() as es2:
            CW = min(512, D)
            NB = D // CW
            pd = es2.enter_context(nc.psum_tensor(uname("fpd"), [128, 2, 4, 512], F32))
            ft = A.alloc([128, 3, CW], F32)
            junk = A.alloc([128, CW], F32)
            ssq = A.alloc([128, 4, NB], F32)
            PSZ = (2 * QE) // CW
            pieces = [(f0, min(FC, f0 + PSZ)) for f0 in range(0, FC, PSZ)]
            n = 0
            np_ = 0
            for j in range(NB):
                pb = j % 2
                for (f0, f1) in pieces:
                    q0 = 2 * (np_ % 2)
                    np_ += 1
                    w = load_w(C, C.dr["w_down"][f0 * 128:f1 * 128, j * CW:(j + 1) * CW], q0, 2, f1 - f0, CW)
                    for tb in range(4):
                        def mm(q):
                            for f in range(f0, f1):
                                ins = q.matmul(pd[:, pb, tb, 0:CW], lhsT=act2[:, f, tb * 128:(tb + 1) * 128], rhs=w[:, f - f0, :], start=(f == 0), stop=(f == FC - 1))
                            return ins
                        S.op("pe", mm, reads=wkeys(q0, 2) + ["act2"], writes=[f"fpd{pb}{tb}"])
                for tb in range(4):
                    s = tile * 4 + tb
                    r = n % 3
                    n += 1
                    S.op("dve", lambda q: q.tensor_copy(out=ft[:, r], in_=pd[:, pb, tb, 0:CW]), writes=[f"fpd{pb}{tb}", f"fft{r}"])
                    S.op("act", lambda q: q.activation(out=junk[:], in_=pd[:, pb, tb, 0:CW], func=AF.Square, accum_out=ssq[:, tb, j:j + 1]), writes=[f"fpd{pb}{tb}", "fjunk", "fssq"])
                    S.dma("sp", lambda q: q.dma_start(out=C.dr["f_scr"][s * 128:(s + 1) * 128, j * CW:(j + 1) * CW], in_=ft[:, r]), reads=[f"fft{r}"], writes=[f"fscr{s}"], semkey=f"fst{r}")
            for j in range(1, NB):
                S.op("dve", lambda q: q.tensor_tensor(out=ssq[:, :, 0], in0=ssq[:, :, 0], in1=ssq[:, :, j], op=ALU.add), reads=["fssq"], writes=["fssq"])
            S.op("act", lambda q: q.activation(out=rstd3[:], in_=ssq[:, :, 0], func=AF.Sqrt, scale=1.0 / D, bias=C.epsb[:]), reads=["fssq", "epsb"], writes=["rstd3p"])
            S.op("dve", lambda q: q.reciprocal(out=rstd3[:], in_=rstd3[:]), writes=["rstd3p"])
            S.barrier()
    A.release(_m0)
    with contextlib.ExitStack() as es3:
        frow = A.alloc([128, 2, D], F32)
        x1row = A.alloc([128, 2, D], F32)
        gpf = A.alloc([128, D], F32)
        S.dma("sp", lambda q: q.dma_start(out=gpf[:], in_=C.dr["gpf_b"]), writes=["gpf"], semkey="gpm")
        for tb in range(4):
            s = tile * 4 + tb
            r = tb % 2
            S.dma("sp", lambda q: q.dma_start(out=frow[:, r], in_=C.dr["f_scr"][s * 128:(s + 1) * 128, :]), reads=[f"fscr{s}"], writes=[f"frow{r}"], semkey=f"frow{r}")
            S.dma("sp", lambda q: q.dma_start(out=x1row[:, r], in_=outv[s]), reads=[f"out{s}"], writes=[f"x1row{r}"], semkey=f"x1row{r}")
            S.op("dve", lambda q: q.scalar_tensor_tensor(out=frow[:, r], in0=frow[:, r], scalar=C.rstd3p[:, tb:tb + 1], in1=gpf[:], op0=ALU.mult, op1=ALU.mult), reads=["gpf", "rstd3p"], writes=[f"frow{r}"])
            S.op("dve", lambda q: q.tensor_tensor(out=frow[:, r], in0=frow[:, r], in1=x1row[:, r], op=ALU.add), reads=[f"x1row{r}"], writes=[f"frow{r}"])
            S.dma("sp", lambda q: q.dma_start(out=outv[s], in_=frow[:, r]), reads=[f"frow{r}"], writes=[f"out{s}"], semkey=f"ost{r}", is_output=True)
        S.barrier()
    A.release(_m)


S5_IN = {"a_ri": lambda: [128, 3, G], "b_ri": lambda: [128, 2, G, H], "c_ri": lambda: [128, G, H],
         "a_gp": lambda: [128, 3, G // 2], "c_gp": lambda: [128, 2, G // 2, H], "d_sh": lambda: [128, G],
         "ident": lambda: [128, 128]}


def build_program(debug=()):
    nc = bass.Bass("TRN2", target_bir_lowering=False)
    dr = {}
    def din(name, shape):
        dr[name] = nc.dram_tensor(name, shape, F32, kind="ExternalInput").ap()
    din("xT", [D, TOK]); din("xTp", [D, TOK]); din("x_nat", [TOK, D])
    din("w_in", [D, DS + 2 * DC]); din("w_glu", [DS, DS]); din("w_out", [DS + DC, D])
    din("w_gate", [D, DFF]); din("w_up", [D, DFF]); din("w_down", [DFF, D])
    din("vec", [128, NVEC()]); din("gpm_b", [128, D]); din("gpf_b", [128, D])
    for k, f in S5_IN.items():
        din(k, f())
    def dscr(name, shape, dt=F32):
        kind = "ExternalOutput" if name in debug else "Internal"
        dr[name] = nc.dram_tensor(name, shape, dt, kind=kind).ap()
    dscr("s5w_m", [128, G, 2, 128], BF16); dscr("s5w_2", [128, G // 2, 2 * 128], BF16)
    dscr("convT", [DC, TOK]); dscr("m_scr", [TOK, D]); dscr("f_scr", [TOK, D])
    dscr("mix_scr", [128, KC, TOK], BF16); dscr("yb_scr", [128, NSC, TOK], BF16)
    for name, shape, dt in (("dbg_sinit", [128, 2, G // 2], F32), ("dbg_xn", [128, KC, TOK], BF16), ("dbg_h2T", [128, KC, 512], BF16), ("dbg_sp", [128, 2, G // 2, 128], BF16)):
        if name in debug:
            dr[name] = nc.dram_tensor(name, shape, dt, kind="ExternalOutput").ap()
    dr["out"] = nc.dram_tensor("out", [TOK, D], F32, kind="ExternalOutput").ap()

    with contextlib.ExitStack() as es:
        S = Sched(nc, es, n_dma_sems=36)
        C = Ctx()
        C.nc, C.S, C.dr = nc, S, dr
        sb = lambda name, shape, dt=F32: es.enter_context(nc.sbuf_tensor("sb_" + name, shape, dt))
        C.ident = sb("ident", [128, 128]); C.identb = sb("identb", [128, 128], BF16); C.ones = sb("ones", [128, 128])
        C.epsb = sb("epsb", [128, 1]); C.A8 = sb("A8", [128, 2, G // 2]); C.Sinit = sb("Sinit", [128, 2, G // 2])
        C.xh = sb("xh", [128, KC, 32], BF16); C.vec = sb("vec", [128, NVEC()]); C.rstd3p = sb("rstd3p", [128, 4])
        off = {}
        o = 0
        for name, n in VEC_LAYOUT():
            off[name] = (o, n); o += n
        vs = lambda name: C.vec[:, off[name][0]: off[name][0] + off[name][1]]
        C.gpre, C.g2, C.bglu, C.gs, C.gc, C.lng, C.lnb, C.bdw = (vs(n) for n in ("gpre", "g2", "bglu", "gs", "gc", "lng", "lnb", "bdw"))
        C.wdw = vs("wdw").rearrange("p (c k) -> p c k", k=31)
        S.dma("sp", lambda q: q.dma_start(out=C.vec[:], in_=dr["vec"]), writes=["vec"], semkey="c0")
        S.dma("sp", lambda q: q.dma_start(out=C.ident[:], in_=dr["ident"]), writes=["ident"], semkey="c1")
        S.op("dve", lambda q: q.tensor_copy(out=C.identb[:], in_=C.ident[:]), reads=["ident"], writes=["identb"])
        S.op("dve", lambda q: q.memset(C.ones[:], 1.0), writes=["ones"])
        S.op("dve", lambda q: q.memset(C.epsb[:], EPS), writes=["epsb"])
        S.op("dve", lambda q: q.memset(C.Sinit[:], 0.0), writes=["Sinit"])
        S.barrier()

        stage_prep(S, nc, dr, C.A8)
        C.wring = sb("wring", [128, 4 * QE], BF16)
        C.A = Arena(nc, es, ARENA_BYTES)
        A = C.A
        rstd1 = sb("rstd1", [128, 8])
        GP = G // 2

        def mixer_front(xT_dram, tag, own):
            m0 = A.mark()
            xn = A.alloc([128, KC, TOK], BF16)
            stage_x(C, xT_dram, xn, tag)
            if not own:
                S.op("dve", lambda q: q.tensor_copy(out=C.xh[:].rearrange("p k (s c) -> p k s c", c=4),
                                                    in_=xn.rearrange("p k (s c) -> p k s c", c=128)[:, :, :, 124:128]), writes=["xh"])
            else:
                if "dbg_xn" in debug:
                    S.dma("sp", lambda q: q.dma_start(out=dr["dbg_xn"], in_=xn), writes=["dbgx"], semkey="dbg", is_output=True)
                    S.barrier()
                stage_c(C, xn)
            Utok = A.alloc([128, G, 8, 16], BF16, hi=True)
            u_proj(C, xn, Utok)
            A.release((m0[0], A.hi))
            Ug = A.alloc([128, G, 128], BF16)
            mUg = A.mark()
            Xs = A.alloc([128, 2, GP, 128], F32)
            u_states(C, Utok, Ug, Xs)
            A.hi = m0[1]
            Sp = A.alloc([128, 2, GP, 128], BF16, hi=True) if own else None
            s5_scan(C, Xs)
            if own:
                S.op("dve", lambda q: q.tensor_copy(out=Sp[:, :, :, 1:128], in_=Xs[:, :, :, 0:127]), writes=["Sp"])
                S.op("dve", lambda q: q.tensor_copy(out=Sp[:, :, :, 0], in_=C.Sinit[:]), writes=["Sp"])
            else:
                S.op("dve", lambda q: q.tensor_copy(out=C.Sinit[:], in_=Xs[:, :, :, 127]), writes=["Sinit"])
            S.barrier()
            if own:
                A.release(mUg)
                if "dbg_sp" in debug:
                    S.dma("sp", lambda q: q.dma_start(out=dr["dbg_sp"], in_=Sp), writes=["dbgp"], semkey="dbg", is_output=True)
                    S.barrier()
                stage_y(C, Ug, Sp)
            A.release(m0)

        mixer_front(dr["xTp"], "p", False)
        if "dbg_sinit" in debug:
            S.dma("sp", lambda q: q.dma_start(out=dr["dbg_sinit"], in_=C.Sinit[:]), writes=["dbgs"], semkey="dbg", is_output=True)
            S.barrier()
        mixer_front(dr["xT"], "o", True)
        stage_g(C)
        stage_o1(C, rstd1)
        for tile in range(2):
            m0 = A.mark()
            h2T = A.alloc([128, KC, 512], BF16)
            stage_o2(C, tile, rstd1, h2T)
            if "dbg_h2T" in debug and tile == 0:
                S.dma("sp", lambda q: q.dma_start(out=dr["dbg_h2T"], in_=h2T), writes=["dbgh"], semkey="dbg", is_output=True)
                S.barrier()
            stage_ffn(C, tile, h2T)
            A.release(m0)
        S.finish()
        C.ninst = S.ninst
    return nc


ARENA_BYTES = 135 * 1024


def VEC_LAYOUT():
    return [("gpre", KC), ("g2", KC), ("bglu", NSC), ("gs", NSC), ("gc", NCC), ("lng", NCC), ("lnb", NCC), ("bdw", NCC), ("wdw", NCC * 31)]


def NVEC():
    return sum(n for _, n in VEC_LAYOUT())


def colvec(v):
    v = np.asarray(v, np.float32).reshape(-1)
    return np.ascontiguousarray(v.reshape(-1, 128).T)


def smajor_T(xtok):
    t = xtok.reshape(TOK // 8, 8, -1)
    return np.ascontiguousarray(t.transpose(2, 1, 0).reshape(xtok.shape[1], TOK))


def host_shared(inp):
    sh = {}
    for k, src in (("w_in", "w_in"), ("w_glu", "ssm_w_glu"), ("w_out", "w_out"), ("w_gate", "w_gate"), ("w_up", "w_up"), ("w_down", "w_down")):
        sh[k] = np.ascontiguousarray(np.asarray(inp[src], np.float32)[0])
    wdw = np.asarray(inp["conv_w_dw"], np.float32)[0]
    wdw_l = np.ascontiguousarray(wdw.T.reshape(NCC, 128, 31).transpose(1, 0, 2)).reshape(128, NCC * 31)
    parts = {"gpre": colvec(inp["ln_pre_mix"][0]), "g2": colvec(inp["ln_pre_ffn"][0]), "bglu": colvec(inp["ssm_b_glu"][0]),
             "gs": colvec(inp["norm_ssm_out"][0]), "gc": colvec(inp["norm_conv_out"][0]), "lng": colvec(inp["conv_ln_g"][0]),
             "lnb": colvec(inp["conv_ln_b"][0]), "bdw": colvec(inp["conv_b_dw"][0]), "wdw": wdw_l}
    sh["vec"] = np.ascontiguousarray(np.concatenate([parts[n] for n, _ in VEC_LAYOUT()], axis=1))
    sh["gpm_b"] = np.ascontiguousarray(np.broadcast_to(np.asarray(inp["ln_post_mix"], np.float32)[0][None, :], (128, D)))
    sh["gpf_b"] = np.ascontiguousarray(np.broadcast_to(np.asarray(inp["ln_post_ffn"], np.float32)[0][None, :], (128, D)))
    sh.update(host_s5_layouts(inp))
    return sh


def host_core_inputs(x, core):
    b, half = core // 2, core % 2
    own = np.asarray(x[b, half * TOK:(half + 1) * TOK], np.float32)
    prev = np.asarray(x[b, 0:TOK], np.float32) if half == 1 else np.zeros((TOK, x.shape[2]), np.float32)
    return {"xT": smajor_T(own), "xTp": smajor_T(prev), "x_nat": np.ascontiguousarray(own)}


_NC_CACHE = {}


def kernel(**inputs):
    x = np.asarray(inputs["x"], np.float32)
    sh = host_shared(inputs)
    if "nc" not in _NC_CACHE:
        _NC_CACHE["nc"] = build_program()
    nc = _NC_CACHE["nc"]
    in_maps = []
    for core in range(8):
        m = dict(sh)
        m.update(host_core_inputs(x, core))
        in_maps.append(m)
    res = run_bass_kernel_spmd(nc, in_maps, core_ids=list(range(8)))
    out = np.empty((BATCH, SEQ, D), np.float32)
    for core in range(8):
        b, half = core // 2, core % 2
        out[b, half * TOK:(half + 1) * TOK] = res.results[core]["out"]
    return out
```
